# Optimizing a Trainium2 kernel written in Bass

```python
import jax, jax.numpy as jnp
from jax import lax
import numpy as np

D_MODEL = 1024
BATCH = 2
SEQ = 16384
DEPTH = 2

CHUNK = 64
Q_BLOCK = 128
D_MIX = D_MODEL
N_MIXERS = 4
GROUP_WIDTH = D_MIX // N_MIXERS
HEAD_DIM = 64
N_HEADS = GROUP_WIDTH // HEAD_DIM
SSD_STATE = 128
CONV_WIDTH = 4
SSD_XBC = GROUP_WIDTH + 2 * SSD_STATE
D_PROJ = 12 * GROUP_WIDTH + SSD_XBC + 2 * N_HEADS
D_FF = ((8 * D_MODEL // 3 + 255) // 256) * 256
PLE_DIM = 256
RMS_EPS = 1e-6
ROPE_BASE = 10000.0

kernel_name = "hybrid_fox_retnet_ssd_hgrn2_macaron"


def rmsnorm(x, w):
    xf = x.astype(jnp.float32)
    y = xf * lax.rsqrt(jnp.mean(xf * xf, axis=-1, keepdims=True) + RMS_EPS)
    return (y * w.astype(jnp.float32)).astype(x.dtype)


def head_rmsnorm(t, w):
    B, S = t.shape[0], t.shape[1]
    return rmsnorm(t, w.reshape(N_HEADS, HEAD_DIM)).reshape(B, S, -1)


def swiglu(h, w_up, w_down):
    gate, up = jnp.split(h @ w_up, 2, axis=-1)
    return (jax.nn.silu(gate) * up) @ w_down


def rotary(t, pos):
    half = t.shape[-1] // 2
    inv_freq = ROPE_BASE ** (-jnp.arange(half, dtype=jnp.float32) / half)
    ang = pos[:, None] * inv_freq[None, :]
    cos, sin = jnp.cos(ang), jnp.sin(ang)
    t1, t2 = t[..., :half].astype(jnp.float32), t[..., half:].astype(jnp.float32)
    return jnp.concatenate([t1 * cos - t2 * sin, t1 * sin + t2 * cos], axis=-1)


def causal_depthwise_conv(x, w, b):
    K, C = w.shape
    y = lax.conv_general_dilated(x, w[:, None, :], window_strides=(1,), padding=[(K - 1, 0)],
                                 dimension_numbers=("NWC", "WIO", "NWC"), feature_group_count=C)
    return y + b


def _to_chunks(t):
    B, H, S, d = t.shape
    return jnp.moveaxis(t.reshape(B, H, S // CHUNK, CHUNK, d), 2, 0)


def chunked_linear_recurrence(q, k, v, log_decay):
    out_dtype = v.dtype
    B, H, S, dk = q.shape
    dv = v.shape[-1]
    scalar_decay = log_decay.shape[-1] == 1
    qc, kc, vc, gc = (_to_chunks(t.astype(jnp.float32)) for t in (q, k, v, log_decay))
    bc = jnp.cumsum(gc, axis=-2)
    causal = jnp.tril(jnp.ones((CHUNK, CHUNK), dtype=bool))

    def step(state, blk):
        qi, ki, vi, bi = blk
        b_last = bi[..., -1:, :]
        if scalar_decay:
            seg = bi[..., :, None, 0] - bi[..., None, :, 0]
            decay = jnp.exp(jnp.where(causal, seg, -jnp.inf))
            scores = jnp.einsum("bhtd,bhsd->bhts", qi, ki) * decay
        else:
            seg = bi[..., :, None, :] - bi[..., None, :, :]
            decay = jnp.exp(jnp.where(causal[:, :, None], seg, -jnp.inf))
            scores = jnp.einsum("bhtd,bhsd,bhtsd->bhts", qi, ki, decay)
        out = (jnp.einsum("bhts,bhse->bhte", scores, vi)
               + jnp.einsum("bhtd,bhde->bhte", qi * jnp.exp(bi), state))
        new_state = (jnp.exp(b_last[..., 0, :])[..., :, None] * state
                     + jnp.einsum("bhsd,bhse->bhde", ki * jnp.exp(b_last - bi), vi))
        return new_state, out

    state0 = jnp.zeros((B, H, dk, dv), jnp.float32)
    _, out = lax.scan(step, state0, (qc, kc, vc, bc))
    return jnp.moveaxis(out, 0, 2).reshape(B, H, S, dv).astype(out_dtype)


def forgetting_attention(q, k, v, log_f):
    out_dtype = v.dtype
    B, H, S, d = q.shape
    q, k, v = (t.astype(jnp.float32) for t in (q, k, v))
    c = jnp.cumsum(log_f.astype(jnp.float32), axis=-1)
    nq = S // Q_BLOCK
    q_blocks = jnp.moveaxis(q.reshape(B, H, nq, Q_BLOCK, d), 2, 0)
    c_blocks = jnp.moveaxis(c.reshape(B, H, nq, Q_BLOCK), 2, 0)
    key_pos = jnp.arange(S)
    scale = HEAD_DIM ** -0.5

    def block(args):
        qi, ci, start = args
        s = jnp.einsum("bhqd,bhkd->bhqk", qi, k) * scale + ci[..., :, None] - c[..., None, :]
        q_pos = start + jnp.arange(Q_BLOCK)
        s = jnp.where(q_pos[:, None] >= key_pos[None, :], s, -jnp.inf)
        return jnp.einsum("bhqk,bhkd->bhqd", jax.nn.softmax(s, axis=-1), v)

    out = lax.map(block, (q_blocks, c_blocks, jnp.arange(nq) * Q_BLOCK))
    return jnp.moveaxis(out, 0, 2).reshape(B, H, S, d).astype(out_dtype)


def token_mixer(u, lb, w_in, fox_f_bias, ret_norm, conv_w, conv_b, dt_bias, a_log, ssd_d,
                ssd_norm, hgrn_norm, w_out):
    B, S, _ = u.shape
    proj = u @ w_in
    sizes = ((GROUP_WIDTH,) * 3 + (N_HEADS,) + (GROUP_WIDTH,) * 4
             + (GROUP_WIDTH, SSD_XBC, N_HEADS) + (GROUP_WIDTH,) * 4)
    splits = np.cumsum(sizes)[:-1].tolist()
    (fq, fk, fv, ff, rq, rk, rv, rg, sz, sxbc, sdt, hq, hf, hi, hg) = jnp.split(proj, splits, axis=-1)

    def heads(t):
        return t.reshape(B, S, N_HEADS, -1).transpose(0, 2, 1, 3)

    def merge(t):
        return t.transpose(0, 2, 1, 3).reshape(B, S, -1)

    log_f = jax.nn.log_sigmoid((ff + fox_f_bias).astype(jnp.float32)).transpose(0, 2, 1)
    y_fox = merge(forgetting_attention(heads(fq), heads(fk), heads(fv), log_f))

    pos = jnp.arange(S, dtype=jnp.float32)
    rq_h = rotary(heads(rq), pos)
    rk_h = rotary(heads(rk), pos) * (HEAD_DIM ** -0.5)
    log_gamma = jnp.log1p(-jnp.exp2(-5.0 - jnp.arange(N_HEADS, dtype=jnp.float32)))
    ret_decay = jnp.broadcast_to(log_gamma[None, :, None, None], (B, N_HEADS, S, 1))
    o_ret = chunked_linear_recurrence(rq_h, rk_h, heads(rv), ret_decay)
    y_ret = head_rmsnorm(o_ret.transpose(0, 2, 1, 3), ret_norm) * jax.nn.silu(rg)

    xbc = jax.nn.silu(causal_depthwise_conv(sxbc, conv_w, conv_b))
    xs, bm, cm = jnp.split(xbc, [GROUP_WIDTH, GROUP_WIDTH + SSD_STATE], axis=-1)
    dt = jax.nn.softplus((sdt + dt_bias).astype(jnp.float32))
    a = -jnp.exp(a_log.astype(jnp.float32))
    dt_h = dt.transpose(0, 2, 1)[..., None]
    xs_h = heads(xs)
    q_ssd = jnp.broadcast_to(cm[:, None], (B, N_HEADS, S, SSD_STATE))
    k_ssd = jnp.broadcast_to(bm[:, None], (B, N_HEADS, S, SSD_STATE))
    o_ssd = (chunked_linear_recurrence(q_ssd, k_ssd, xs_h * dt_h, dt_h * a[:, None, None])
             + ssd_d[:, None, None] * xs_h)
    y_ssd = rmsnorm(merge(o_ssd) * jax.nn.silu(sz), ssd_norm)

    lb = lb.astype(jnp.float32)
    log_f_h = jnp.logaddexp(jnp.log(jnp.maximum(lb, 0.0)),
                            jnp.log1p(-lb) + jax.nn.log_sigmoid(hf.astype(jnp.float32)))
    k_h = -jnp.expm1(log_f_h)
    o_hg = chunked_linear_recurrence(heads(hq), heads(k_h), heads(hi), heads(log_f_h))
    y_hg = head_rmsnorm(o_hg.transpose(0, 2, 1, 3), hgrn_norm) * jax.nn.silu(hg)

    y = jnp.concatenate([y_fox, y_ret, y_ssd, y_hg], axis=-1)
    return (y @ w_out).astype(u.dtype)


def setup_inputs(seed: int = 0) -> dict:
    key = jax.random.key(seed)
    ks = jax.random.split(key, 26)
    f32 = jnp.float32

    def nrm(k, shape, scale):
        return jax.random.normal(k, shape, f32) * scale

    def gain(k, shape):
        return 1.0 + 0.02 * jax.random.normal(k, shape, f32)

    dt_init = jnp.exp(jax.random.uniform(ks[11], (DEPTH, N_HEADS), f32,
                                         np.log(1e-3), np.log(1e-1)))
    return {
        "x": nrm(ks[0], (BATCH, SEQ, D_MODEL), 1.0),
        "p": nrm(ks[1], (DEPTH, BATCH, SEQ, PLE_DIM), 1.0),
        "ffn1_norm": gain(ks[2], (DEPTH, D_MODEL)),
        "ffn1_w_up": nrm(ks[3], (DEPTH, D_MODEL, 2 * D_FF), D_MODEL ** -0.5),
        "ffn1_w_down": nrm(ks[4], (DEPTH, D_FF, D_MODEL), D_FF ** -0.5),
        "mix_norm": gain(ks[5], (DEPTH, D_MODEL)),
        "w_in": nrm(ks[6], (DEPTH, D_MODEL, D_PROJ), D_MODEL ** -0.5),
        "fox_f_bias": 2.0 + nrm(ks[7], (DEPTH, N_HEADS), 0.5),
        "ret_norm": gain(ks[8], (DEPTH, GROUP_WIDTH)),
        "conv_w": nrm(ks[9], (DEPTH, CONV_WIDTH, SSD_XBC), CONV_WIDTH ** -0.5),
        "conv_b": nrm(ks[10], (DEPTH, SSD_XBC), 0.02),
        "dt_bias": dt_init + jnp.log(-jnp.expm1(-dt_init)),
        "a_log": jnp.log(jax.random.uniform(ks[12], (DEPTH, N_HEADS), f32, 1.0, 16.0)),
        "ssd_d": 1.0 + nrm(ks[13], (DEPTH, N_HEADS), 0.1),
        "ssd_norm": gain(ks[14], (DEPTH, GROUP_WIDTH)),
        "hgrn_lower_bounds": 1.0 + nrm(ks[15], (DEPTH, GROUP_WIDTH), 0.1),
        "hgrn_norm": gain(ks[16], (DEPTH, GROUP_WIDTH)),
        "w_out": nrm(ks[17], (DEPTH, D_MIX, D_MODEL), D_MIX ** -0.5),
        "ffn2_norm": gain(ks[18], (DEPTH, D_MODEL)),
        "ffn2_w_up": nrm(ks[19], (DEPTH, D_MODEL, 2 * D_FF), D_MODEL ** -0.5),
        "ffn2_w_down": nrm(ks[20], (DEPTH, D_FF, D_MODEL), D_FF ** -0.5),
        "ple_norm": gain(ks[21], (DEPTH, D_MODEL)),
        "ple_w_gate": nrm(ks[22], (DEPTH, D_MODEL, D_MODEL), D_MODEL ** -0.5),
        "ple_w_proj": nrm(ks[23], (DEPTH, PLE_DIM, D_MODEL), PLE_DIM ** -0.5),
        "final_norm": gain(ks[24], (D_MODEL,)),
    }


def reference(x, p, ffn1_norm, ffn1_w_up, ffn1_w_down, mix_norm, w_in, fox_f_bias, ret_norm,
              conv_w, conv_b, dt_bias, a_log, ssd_d, ssd_norm, hgrn_lower_bounds, hgrn_norm,
              w_out, ffn2_norm, ffn2_w_up, ffn2_w_down, ple_norm, ple_w_gate, ple_w_proj,
              final_norm):
    lbs = jnp.cumsum(jax.nn.softmax(hgrn_lower_bounds.astype(jnp.float32), axis=0), axis=0)
    lbs = lbs - lbs[0]
    h = x
    for i in range(DEPTH):
        h = h + 0.5 * swiglu(rmsnorm(h, ffn1_norm[i]), ffn1_w_up[i], ffn1_w_down[i])
        h = h + token_mixer(rmsnorm(h, mix_norm[i]), lbs[i], w_in[i], fox_f_bias[i], ret_norm[i],
                            conv_w[i], conv_b[i], dt_bias[i], a_log[i], ssd_d[i], ssd_norm[i],
                            hgrn_norm[i], w_out[i])
        h = h + 0.5 * swiglu(rmsnorm(h, ffn2_norm[i]), ffn2_w_up[i], ffn2_w_down[i])
        gate = jax.nn.sigmoid(rmsnorm(h, ple_norm[i]) @ ple_w_gate[i])
        h = h + gate * (p[i] @ ple_w_proj[i])
    return rmsnorm(h, final_norm)
```

```python
from contextlib import ExitStack
import numpy as np
import concourse.bass as bass
import concourse.mybir as mybir
from concourse.bass_utils import run_bass_kernel_spmd

F32 = mybir.dt.float32
BF16 = mybir.dt.bfloat16
AF = mybir.ActivationFunctionType
ALU = mybir.AluOpType
AX = mybir.AxisListType

S = 16384
D = 1024
DFF = 2816
NFM = 834
NTM = 192
EPS = 1e-6
TG = 512
NEG = -30000.0


class Em:
    NDMA = 8

    def __init__(self, nc, es, same_engine_sync=True):
        self.nc = nc
        self.eng = {"pe": nc.tensor, "act": nc.scalar, "dve": nc.vector, "pool": nc.gpsimd, "sp": nc.sync}
        self.sem = {}
        self.cnt = {}
        for e in ("pe", "act", "dve", "pool"):
            self.sem[e] = es.enter_context(nc.semaphore("sem_" + e))
            self.cnt[e] = 0
        self.dsem = {}
        self.didx = {}
        for q in ("sp", "pool", "act"):
            self.dsem[q] = [es.enter_context(nc.semaphore("dsem_%s%d" % (q, i))) for i in range(self.NDMA)]
            self.didx[q] = 0
        self.waited = {}
        self.last_w = {}
        self.readers = {}
        self.same = same_engine_sync
        self.nwaits = 0
        self.nins = 0

    def _wait(self, e, ev):
        sem, val, src = ev
        if src == e and (e == "pe" or not self.same):
            return
        k = (e, id(sem))
        if self.waited.get(k, 0) >= val:
            return
        self.eng[e].wait_ge(sem, val)
        self.waited[k] = val
        self.nwaits += 1

    @staticmethod
    def _k(x):
        if isinstance(x, (str, tuple, int)):
            return x
        return x.name

    def _deps(self, e, reads, writes):
        evs = []
        for r in reads:
            if r in self.last_w:
                evs.append(self.last_w[r])
        for w in writes:
            if w in self.last_w:
                evs.append(self.last_w[w])
            for ev in self.readers.get(w, {}).values():
                evs.append(ev)
        for ev in evs:
            self._wait(e, ev)

    def _record(self, ev, reads, writes):
        for w in writes:
            self.last_w[w] = ev
            self.readers[w] = {}
        for r in reads:
            if r in writes:
                continue
            d = self.readers.setdefault(r, {})
            k = id(ev[0])
            if k not in d or d[k][1] < ev[1]:
                d[k] = ev

    def op(self, e, fn, reads=(), writes=()):
        reads = [self._k(r) for r in reads]
        writes = [self._k(w) for w in writes]
        self._deps(e, reads, writes)
        ins = fn(self.eng[e])
        self.cnt[e] += 1
        ins.then_inc(self.sem[e], 1)
        self.nins += 1
        self._record((self.sem[e], self.cnt[e], e), reads, writes)

    def dma(self, q, out, in_, reads=(), writes=(), **kw):
        reads = [self._k(r) for r in reads]
        writes = [self._k(w) for w in writes]
        i = self.didx[q]
        self.didx[q] += 1
        sem = self.dsem[q][i % self.NDMA]
        rnd = i // self.NDMA
        if rnd > 0:
            self._wait(q, (sem, 16 * rnd, "dma_" + q))
        self._deps(q, reads, writes)
        self.eng[q].dma_start(out=out, in_=in_, **kw).then_inc(sem, 16)
        self.nins += 1
        self._record((sem, 16 * (rnd + 1), "dma_" + q), reads, writes)

    def finish(self, e="sp"):
        for q in self.dsem:
            n = self.didx[q]
            for k in range(min(n, self.NDMA)):
                last_i = ((n - 1 - k) // self.NDMA) * self.NDMA + k
                self._wait(e, (self.dsem[q][k], 16 * (last_i // self.NDMA + 1), "dma_" + q))


class Cx:
    def __init__(self, nc, es):
        self.nc = nc
        self.es = es
        self.em = Em(nc, es)
        self.n = 0

    def sb(self, name, shape, dt):
        return self.es.enter_context(self.nc.sbuf_tensor("sb_" + name, list(shape), dt))

    def ps(self, name, shape, dt):
        return self.es.enter_context(self.nc.psum_tensor("ps_" + name, list(shape), dt))

    def din(self, name, shape, dt=F32):
        return self.nc.dram_tensor(name, list(shape), dt, kind="ExternalInput").ap()

    def dout(self, name, shape, dt=F32):
        return self.nc.dram_tensor(name, list(shape), dt, kind="ExternalOutput").ap()


class WStream:
    def __init__(self, cx, blocks, nbuf=4, kch=8, name="wst"):
        self.cx = cx
        self.blocks = blocks
        self.kch = kch
        self.bufs = [cx.sb("%s%d" % (name, i), [128, kch, 512], BF16) for i in range(nbuf)]
        self.issued = 0
        self.nbuf = nbuf

    def _issue(self, i):
        ap = self.blocks[i]
        ncols = ap.shape[1]
        buf = self.bufs[i % self.nbuf]
        self.cx.em.dma("pool", buf[:, :, 0:ncols], ap.rearrange("(k p) c -> p k c", p=128), writes=[buf])

    def get(self, i):
        while self.issued < min(len(self.blocks), i + self.nbuf - 1):
            self._issue(self.issued)
            self.issued += 1
        return self.bufs[i % self.nbuf]


def load_bcast(cx, name, vec_ap, n, q="sp"):
    t = cx.sb(name, [128, n], F32)
    cx.em.dma(q, t[:], vec_ap.partition_broadcast(128), writes=[t])
    return t


def rms_tm(cx, x_ap, xkeys, gain_ap, gkeys, out_ap, okeys, width, scr):
    em = cx.em
    junk, ssq, rstd = scr
    em.op("act", lambda e: e.activation(out=junk[:, 0:width], in_=x_ap, func=AF.Square, accum_out=ssq[:, 0:1]),
          reads=xkeys, writes=[junk, ssq])
    em.op("dve", lambda e: e.tensor_scalar(out=rstd[:, 0:1], in0=ssq[:, 0:1], scalar1=1.0 / width, scalar2=EPS,
                                           op0=ALU.mult, op1=ALU.add), reads=[ssq], writes=[rstd])
    em.op("act", lambda e: e.activation(out=rstd[:, 0:1], in_=rstd[:, 0:1], func=AF.Sqrt), reads=[rstd], writes=[rstd])
    em.op("dve", lambda e: e.reciprocal(out=rstd[:, 0:1], in_=rstd[:, 0:1]), reads=[rstd], writes=[rstd])
    em.op("dve", lambda e: e.scalar_tensor_tensor(out=out_ap, in0=x_ap, scalar=rstd[:, 0:1], in1=gain_ap,
                                                  op0=ALU.mult, op1=ALU.mult),
          reads=list(xkeys) + [rstd] + list(gkeys), writes=okeys)


def transpose_to_T(cx, u, uT, t, identb, psT, nk=8):
    em = cx.em
    for k in range(nk):
        em.op("pe", lambda e, k=k: e.transpose(psT[:, k * 128:(k + 1) * 128], u[:, k * 128:(k + 1) * 128], identb[:]),
              reads=[u, identb], writes=[psT])
    em.op("act", lambda e: e.copy(out=uT[:, 0:nk, t * 128:(t + 1) * 128],
                                   in_=psT[:, 0:nk * 128].rearrange("p (k c) -> p k c", c=128)),
          reads=[psT], writes=[uT])


class FFN:
    def __init__(self, cx, pfx, norm_ap, wup_ap, wdn_ap, shared):
        self.cx = cx
        em = cx.em
        self.sh = shared
        self.gain = load_bcast(cx, pfx + "gain", norm_ap, D)
        self.wd = cx.sb(pfx + "wd", [128, 22, D], BF16)
        em.dma("pool", self.wd[:], wdn_ap.rearrange("(j p) c -> p j c", p=128), writes=[self.wd])
        self.wup = wup_ap

    def blocks(self):
        bl = []
        for i in range(6):
            nc_ = 512 if i < 5 else 256
            bl.append(self.wup[:, i * 512:i * 512 + nc_])
            bl.append(self.wup[:, DFF + i * 512:DFF + i * 512 + nc_])
        return bl

    def emit(self, hg, ws, wbase):
        cx, em, sh = self.cx, self.cx.em, self.sh
        uT, actT = sh["uT"], sh["actT"]
        for t in range(4):
            u = sh["u"][t % 2]
            rms_tm(cx, hg[:, t, :], [hg], self.gain[:], [self.gain], u[:], [u], D, sh["scr"])
            transpose_to_T(cx, u, uT, t, sh["identb"], sh["psT"])
        for i in range(6):
            wg = ws.get(wbase + 2 * i)
            wu = ws.get(wbase + 2 * i + 1)
            for jj in range(4 if i < 5 else 2):
                j = 4 * i + jj
                pg = sh["pg"][j % 2]
                pu = sh["pu"][j % 2]
                for k in range(8):
                    em.op("pe", lambda e, k=k, jj=jj, pg=pg, wg=wg: e.matmul(
                        pg[:], lhsT=wg[:, k, jj * 128:(jj + 1) * 128], rhs=uT[:, k, :], start=(k == 0), stop=(k == 7)),
                        reads=[wg, uT], writes=[pg])
                for k in range(8):
                    em.op("pe", lambda e, k=k, jj=jj, pu=pu, wu=wu: e.matmul(
                        pu[:], lhsT=wu[:, k, jj * 128:(jj + 1) * 128], rhs=uT[:, k, :], start=(k == 0), stop=(k == 7)),
                        reads=[wu, uT], writes=[pu])
                sg = sh["sg"][j % 2]
                em.op("act", lambda e, pg=pg, sg=sg: e.activation(out=sg[:], in_=pg[:], func=AF.Silu),
                      reads=[pg], writes=[sg])
                em.op("dve", lambda e, j=j, pu=pu, sg=sg: e.tensor_tensor(out=actT[:, j, :], in0=pu[:], in1=sg[:], op=ALU.mult),
                      reads=[pu, sg], writes=[(actT.name, j)])
        akeys = [(actT.name, j) for j in range(22)]
        for t in range(4):
            for hf in range(2):
                pd = sh["pd"][(2 * t + hf) % 2]
                for j in range(22):
                    em.op("pe", lambda e, j=j, t=t, hf=hf, pd=pd: e.matmul(
                        pd[:], lhsT=actT[:, j, t * 128:(t + 1) * 128], rhs=self.wd[:, j, hf * 512:(hf + 1) * 512],
                        start=(j == 0), stop=(j == 21)), reads=akeys + [self.wd], writes=[pd])
                em.op("dve", lambda e, t=t, hf=hf, pd=pd: e.scalar_tensor_tensor(
                    out=hg[:, t, hf * 512:(hf + 1) * 512], in0=pd[:], scalar=0.5, in1=hg[:, t, hf * 512:(hf + 1) * 512],
                    op0=ALU.mult, op1=ALU.add), reads=[pd, hg], writes=[hg])


def ffn_shared(cx):
    sh = {}
    sh["uT"] = cx.sb("uT", [128, 8, TG], BF16)
    sh["actT"] = cx.sb("actT", [128, 22, TG], BF16)
    sh["u"] = [cx.sb("u%d" % i, [128, D], BF16) for i in range(2)]
    sh["sg"] = [cx.sb("sg%d" % i, [128, TG], F32) for i in range(2)]
    sh["scr"] = (cx.sb("junk", [128, D], BF16), cx.sb("ssq", [128, 1], F32), cx.sb("rstd", [128, 1], F32))
    sh["psT"] = cx.ps("psT", [128, D], BF16)
    sh["pg"] = [cx.ps("pg%d" % i, [128, TG], F32) for i in range(2)]
    sh["pu"] = [cx.ps("pu%d" % i, [128, TG], F32) for i in range(2)]
    sh["pd"] = [cx.ps("pd%d" % i, [128, TG], F32) for i in range(2)]
    return sh


def load_ident(cx, ident_ap):
    idf = cx.sb("identf", [128, 128], F32)
    idb = cx.sb("identb", [128, 128], BF16)
    cx.em.dma("sp", idf[:], ident_ap, writes=[idf])
    cx.em.dma("pool", idb[:], ident_ap, writes=[idb])
    return idf, idb


def build_A(tpc):
    nc = bass.Bass("TRN2", target_bir_lowering=False)
    ng = tpc // TG
    with ExitStack() as es:
        cx = Cx(nc, es)
        em = cx.em
        h_in = cx.din("h_in", [tpc, D])
        n1 = cx.din("ffn1_norm", [D])
        wup = cx.din("ffn1_w_up", [D, 2 * DFF])
        wdn = cx.din("ffn1_w_down", [DFF, D])
        mixn = cx.din("mix_norm", [D])
        wfm = cx.din("w_fm", [D, 4 * NFM])
        wtm = cx.din("w_tm", [D, 8 * NTM])
        ident = cx.din("ident", [128, 128])
        h1 = cx.dout("h1", [tpc, D])
        xt = cx.dout("xt", [4 * NFM, tpc])
        xm = cx.dout("xm", [tpc, 4 * NTM])
        gg = cx.dout("gg", [tpc, 4 * NTM])

        sh = ffn_shared(cx)
        idf, idb = load_ident(cx, ident)
        sh["identb"] = idb
        ffn = FFN(cx, "f1", n1, wup, wdn, sh)
        mgain = load_bcast(cx, "mixgain", mixn, D)
        hgs = [cx.sb("hg%d" % i, [128, 4, D], F32) for i in range(2)]
        stg = [cx.sb("stg%d" % i, [128, TG], F32) for i in range(3)]
        nfmb = (4 * NFM + 511) // 512
        fm_blocks = [wfm[:, i * 512:min((i + 1) * 512, 4 * NFM)] for i in range(nfmb)]
        tm_blocks = [wtm[:, i * 512:(i + 1) * 512] for i in range(3)]
        per_g = ffn.blocks() + fm_blocks + tm_blocks
        ws = WStream(cx, per_g * ng)
        nstg = 0
        for g in range(ng):
            hg = hgs[g % 2]
            em.dma("sp", hg[:], h_in[g * TG:(g + 1) * TG, :].rearrange("(t p) c -> p t c", p=128), writes=[hg])
            wbase = g * len(per_g)
            ffn.emit(hg, ws, wbase)
            em.dma("sp", h1[g * TG:(g + 1) * TG, :].rearrange("(t p) c -> p t c", p=128), hg[:], reads=[hg])
            uT = sh["uT"]
            for t in range(4):
                u = sh["u"][t % 2]
                rms_tm(cx, hg[:, t, :], [hg], mgain[:], [mgain], u[:], [u], D, sh["scr"])
                transpose_to_T(cx, u, uT, t, idb, sh["psT"])
            cidx = 0
            for bi in range(nfmb):
                wb = ws.get(wbase + 12 + bi)
                ncols = fm_blocks[bi].shape[1]
                for c0 in range(0, ncols, 128):
                    m = min(128, ncols - c0)
                    pg = sh["pg"][cidx % 2]
                    for k in range(8):
                        em.op("pe", lambda e, k=k, c0=c0, m=m, pg=pg, wb=wb: e.matmul(
                            pg[0:m, :], lhsT=wb[:, k, c0:c0 + m], rhs=uT[:, k, :], start=(k == 0), stop=(k == 7)),
                            reads=[wb, uT], writes=[pg])
                    st = stg[nstg % 3]
                    nstg += 1
                    eng = "act" if cidx % 2 == 0 else "dve"
                    if eng == "act":
                        em.op("act", lambda e, m=m, pg=pg, st=st: e.copy(out=st[0:m, :], in_=pg[0:m, :]), reads=[pg], writes=[st])
                    else:
                        em.op("dve", lambda e, m=m, pg=pg, st=st: e.tensor_copy(out=st[0:m, :], in_=pg[0:m, :]), reads=[pg], writes=[st])
                    r0 = bi * 512 + c0
                    em.dma("sp", xt[r0:r0 + m, g * TG:(g + 1) * TG], st[0:m, :], reads=[st])
                    cidx += 1
            for cb in range(3):
                wb = ws.get(wbase + 12 + nfmb + cb)
                for t in range(4):
                    pu = sh["pu"][(cb * 4 + t) % 2]
                    for k in range(8):
                        em.op("pe", lambda e, k=k, t=t, pu=pu, wb=wb: e.matmul(
                            pu[:], lhsT=uT[:, k, t * 128:(t + 1) * 128], rhs=wb[:, k, :], start=(k == 0), stop=(k == 7)),
                            reads=[wb, uT], writes=[pu])
                    st = stg[nstg % 3]
                    nstg += 1
                    if t % 2 == 0:
                        em.op("act", lambda e, pu=pu, st=st: e.copy(out=st[:], in_=pu[:]), reads=[pu], writes=[st])
                    else:
                        em.op("dve", lambda e, pu=pu, st=st: e.tensor_copy(out=st[:], in_=pu[:]), reads=[pu], writes=[st])
                    r0 = g * TG + t * 128
                    if cb == 0:
                        em.dma("sp", xm[r0:r0 + 128, 0:512], st[:], reads=[st])
                    elif cb == 1:
                        em.dma("sp", xm[r0:r0 + 128, 512:768], st[:, 0:256], reads=[st])
                        em.dma("sp", gg[r0:r0 + 128, 0:256], st[:, 256:512], reads=[st])
                    else:
                        em.dma("sp", gg[r0:r0 + 128, 256:768], st[:], reads=[st])
        em.finish("sp")
        print("A: nins", em.nins, "nwaits", em.nwaits)
    return nc


R_FQ, R_FK, R_RQ, R_RQS, R_RK, R_RKS, R_SX, R_SB, R_SC, R_HQ, R_HF, R_FF, R_SDT = (
    0, 64, 128, 192, 256, 320, 384, 448, 576, 704, 768, 832, 833)


def rec_tile(cx, P, AT, BT, QT, Mt, v, km, Dl, St, dk, nsub, t, otile, ocol, extra=None):
    em = cx.em
    bank, pub, idf = P["core"], P["pU"], P["idf"]
    sc = bank[:, 0:128]
    po = bank[0:64, 128:256]
    pT = bank[:, 256:320]
    L = 128 // nsub
    cs = slice(t * 128, (t + 1) * 128)
    PT, oT = P["PT"], P["oT"]
    em.op("pe", lambda e: e.matmul(sc, lhsT=BT[0:dk, cs], rhs=AT[0:dk, cs], start=True, stop=True),
          reads=[BT, AT], writes=[bank])
    em.op("dve", lambda e: e.tensor_tensor(out=PT[:], in0=sc, in1=Mt, op=ALU.mult), reads=[bank, P["Mkey"]], writes=[PT])
    slots = []
    for c in range(nsub):
        sl = P["uslot"][0] % 8
        P["uslot"][0] += 1
        slots.append(sl)
        em.op("pe", lambda e, c=c, sl=sl: e.matmul(pub[0:dk, sl * 64:(sl + 1) * 64], lhsT=km[:, t * nsub + c, 0:dk],
                                                   rhs=v[:, t, :], start=True, stop=True),
              reads=[km, v], writes=[pub])
    em.op("pe", lambda e: e.matmul(po, lhsT=v[:, t, :], rhs=PT[:], start=True, stop=False), reads=[v, PT], writes=[bank])
    for c in range(nsub):
        cur = St[2]
        s_old, s_new = St[cur], St[1 - cur]
        em.op("pe", lambda e, c=c, s_old=s_old: e.matmul(po[:, c * L:(c + 1) * L], lhsT=s_old[0:dk, :],
                                                         rhs=QT[0:dk, t * 128 + c * L:t * 128 + (c + 1) * L],
                                                         start=False, stop=(c == nsub - 1)),
              reads=[s_old, QT], writes=[bank])
        sl = slots[c]
        em.op("dve", lambda e, c=c, sl=sl, s_old=s_old, s_new=s_new: e.scalar_tensor_tensor(
            out=s_new[0:dk, :], in0=s_old[0:dk, :], scalar=Dl(t, c), in1=pub[0:dk, sl * 64:(sl + 1) * 64],
            op0=ALU.mult, op1=ALU.add), reads=[s_old, pub, P["Dkey"]], writes=[s_new])
        St[2] = 1 - cur
    em.op("act", lambda e: e.copy(out=oT[:], in_=po), reads=[bank], writes=[oT])
    em.op("pe", lambda e: e.transpose(pT, oT[:], idf[0:64, 0:64]), reads=[oT, idf], writes=[bank])
    if extra is None:
        em.op("act", lambda e: e.copy(out=otile[:, t, ocol:ocol + 64], in_=pT), reads=[bank], writes=[otile])
    else:
        em.op("dve", lambda e: e.tensor_tensor(out=otile[:, t, ocol:ocol + 64], in0=pT, in1=extra[:, t, :], op=ALU.add),
              reads=[bank, extra], writes=[otile])


def build_B(S_):
    nc = bass.Bass("TRN2", target_bir_lowering=False)
    NG = S_ // TG
    NKB = S_ // 128
    SEG = S_ // 4
    with ExitStack() as es:
        cx = Cx(nc, es)
        em = cx.em
        xt = cx.din("xt", [4, NFM, SEG])
        xm = cx.din("xm", [4, SEG, NTM])
        ident = cx.din("ident", [128, 128])
        cosT = cx.din("cosT", [64, S_])
        sinsT = cx.din("sinsT", [64, S_])
        fmask = cx.din("fmask", [4, 128, TG])
        retM_d = cx.din("retM", [128, 128])
        retEq_d = cx.din("retEq", [64, TG])
        retkhs_d = cx.din("retkhs", [128, 1])
        retDl_d = cx.din("retDl", [64, 1])
        mneg_d = cx.din("mneg", [128, 128])
        sm128_d = cx.din("sm128", [1, TG])
        hgM_d = cx.din("hgM", [128, 128])
        sm16_d = cx.din("sm16", [64, TG])
        sel_d = cx.din("sel", [128, 8])
        ffb_d = cx.din("fox_fb", [1, 1])
        cwx_d = cx.din("cw_x", [64, 4]); cwb_d = cx.din("cw_b", [128, 4]); cwc_d = cx.din("cw_c", [128, 4])
        cbx_d = cx.din("cb_x", [64, 1]); cbb_d = cx.din("cb_b", [128, 1]); cbc_d = cx.din("cb_c", [128, 1])
        dtb_d = cx.din("dt_bias", [1, 1]); alog_d = cx.din("a_log", [1, 1]); ssdd_d = cx.din("ssd_d", [1])
        hlb_d = cx.din("hlb", [64, 2]); lbsel_d = cx.din("lbsel", [64, 2])
        o_d = cx.dout("o", [S_, 256])

        def sbt(name, shape, dt=F32):
            return cx.sb(name, shape, dt)

        def cload(name, ap, shape, dt=F32, q="sp"):
            t = sbt(name, shape, dt)
            em.dma(q, t[:], ap, writes=[t])
            return t

        idf = cload("idf", ident, [128, 128])
        idb = cload("idb", ident, [128, 128], BF16, q="pool")
        fmk = cload("fmk", fmask.rearrange("j p c -> p j c"), [128, 4, TG])
        retM = cload("retM", retM_d, [128, 128])
        retEq = cload("retEq", retEq_d, [64, TG])
        retkhs = cload("retkhs", retkhs_d, [128, 1])
        retDl = cload("retDl", retDl_d, [64, 1])
        mneg = cload("mneg", mneg_d, [128, 128])
        sm128 = cload("sm128", sm128_d, [1, TG])
        hgM = cload("hgM", hgM_d, [128, 128])
        sm16 = cload("sm16", sm16_d, [64, TG])
        sel = cload("sel", sel_d, [128, 8])
        cwx = cload("cwx", cwx_d, [64, 4]); cwb = cload("cwb", cwb_d, [128, 4]); cwc = cload("cwc", cwc_d, [128, 4])
        cbx = cload("cbx", cbx_d, [64, 1]); cbb = cload("cbb", cbb_d, [128, 1]); cbc = cload("cbc", cbc_d, [128, 1])
        dtb = cload("dtb", dtb_d, [1, 1]); alog = cload("alog", alog_d, [1, 1])
        ssdd = load_bcast(cx, "ssdd", ssdd_d, 1)
        hlb = cload("hlb", hlb_d, [64, 2]); lbsel = cload("lbsel", lbsel_d, [64, 2])

        psS = [cx.ps("psS%d" % i, [128, TG], F32) for i in range(2)]
        psO = cx.ps("psO", [128, TG], F32)
        psX = cx.ps("psX", [128, TG], F32)
        core = cx.ps("core", [128, TG], F32)
        prep = cx.ps("prep", [128, TG], F32)
        bcb = cx.ps("bcb", [128, TG], F32)
        pub = cx.ps("pub", [128, TG], F32)
        P = {"core": core, "pU": pub, "idf": idf, "uslot": [0],
             "PT": sbt("PTr", [128, 128], BF16), "oT": sbt("oTr", [64, 128], F32)}

        a_neg = sbt("a_neg", [1, 1])
        em.op("act", lambda e: e.activation(out=a_neg[:], in_=alog[:], func=AF.Exp), reads=[alog], writes=[a_neg])
        em.op("dve", lambda e: e.tensor_scalar(out=a_neg[:], in0=a_neg[:], scalar1=-1.0, scalar2=None, op0=ALU.mult),
              reads=[a_neg], writes=[a_neg])
        hexp = sbt("hexp", [64, 2]); hsum = sbt("hsum", [64, 1]); lb = sbt("lb", [64, 1]); oml = sbt("oml", [64, 1])
        em.op("act", lambda e: e.activation(out=hexp[:], in_=hlb[:], func=AF.Exp), reads=[hlb], writes=[hexp])
        em.op("dve", lambda e: e.tensor_reduce(out=hsum[:], in_=hexp[:], axis=AX.X, op=ALU.add), reads=[hexp], writes=[hsum])
        em.op("dve", lambda e: e.reciprocal(out=hsum[:], in_=hsum[:]), reads=[hsum], writes=[hsum])
        em.op("dve", lambda e: e.tensor_tensor(out=hexp[:], in0=hexp[:], in1=lbsel[:], op=ALU.mult), reads=[hexp, lbsel], writes=[hexp])
        em.op("dve", lambda e: e.tensor_reduce(out=lb[:], in_=hexp[:], axis=AX.X, op=ALU.add), reads=[hexp], writes=[lb])
        em.op("dve", lambda e: e.tensor_tensor(out=lb[:], in0=lb[:], in1=hsum[:], op=ALU.mult), reads=[lb, hsum], writes=[lb])
        em.op("dve", lambda e: e.tensor_scalar(out=oml[:], in0=lb[:], scalar1=-1.0, scalar2=1.0, op0=ALU.mult, op1=ALU.add),
              reads=[lb], writes=[oml])

        KaT = sbt("KaT", [65, S_], BF16)
        Vaug = sbt("Vaug", [128, NKB, 65], BF16)
        negc = sbt("negc", [128, NKB])
        ones65 = sbt("ones65", [65, TG])
        nfb = sbt("nfb", [65, 1])
        prevc = sbt("prevc", [65, 1])
        em.op("pool", lambda e: e.memset(KaT[64:65, :], 1.0), writes=[KaT])
        em.op("pool", lambda e: e.memset(Vaug[:, :, 64:65], 1.0), writes=[Vaug])
        em.op("pool", lambda e: e.memset(ones65[:], 1.0), writes=[ones65])
        em.op("pool", lambda e: e.memset(prevc[:], 0.0), writes=[prevc])
        em.dma("sp", nfb[64:65, 0:1], ffb_d, writes=[nfb])
        em.op("act", lambda e: e.mul(out=nfb[64:65, :], in_=nfb[64:65, :], mul=-1.0), reads=[nfb], writes=[nfb])
        QaT = sbt("QaT", [65, TG], BF16)
        fq = sbt("fq", [64, TG]); fk = sbt("fk", [64, TG]); lfin = sbt("lfin", [65, TG]); spv = sbt("spv", [65, TG])
        ncr = sbt("ncr", [65, TG]); fv = sbt("fv", [128, 4, 64])
        cbg = sbt("cbg", [128, 1]); biasg = sbt("biasg", [128, NKB])
        PTf = [sbt("PTf%d" % i, [128, TG], BF16) for i in range(2)]
        ftmp = sbt("ftmp", [128, TG])
        OT = sbt("OT", [65, TG]); rec = sbt("rec", [128, 4])
        otiles = [sbt("otile%d" % i, [128, 4, 256]) for i in range(2)]

        def mkstate(name, dk):
            a = sbt(name + "a", [dk, 64]); b = sbt(name + "b", [dk, 64])
            em.op("pool", lambda e: e.memset(a[:], 0.0), writes=[a])
            em.op("pool", lambda e: e.memset(b[:], 0.0), writes=[b])
            return [a, b, 0]
        St_r = mkstate("Sr", 64); St_s = mkstate("Ss", 128); St_h = mkstate("Sh", 64)

        rq = sbt("rq", [64, TG]); rqs = sbt("rqs", [64, TG]); rk = sbt("rk", [64, TG]); rks = sbt("rks", [64, TG])
        cst = sbt("cst", [64, TG]); snt = sbt("snt", [64, TG]); rt1 = sbt("rt1", [64, TG]); rt2 = sbt("rt2", [64, TG])
        rA = sbt("rA", [64, TG], BF16); rB = sbt("rB", [64, TG], BF16); rQ = sbt("rQ", [64, TG]); rqf = sbt("rqf", [64, TG])
        rv = sbt("rv", [128, 4, 64]); rvb = sbt("rvb", [128, 4, 64], BF16); rkm = sbt("rkm", [128, 4, 64], BF16)
        sxin = sbt("sxin", [64, TG + 3]); sbin = sbt("sbin", [128, TG + 3]); scin = sbt("scin", [128, TG + 3])
        sxa = sbt("sxa", [64, TG]); sba = sbt("sba", [128, TG]); sca = sbt("sca", [128, TG])
        xsT = sbt("xsT", [64, TG]); sB = sbt("sB", [128, TG], BF16); sA = sbt("sA", [128, TG], BF16); sCf = sbt("sCf", [128, TG])
        sdt = sbt("sdt", [1, TG]); dtr = sbt("dtr", [1, TG]); gr = sbt("gr", [1, TG]); br = sbt("br", [1, TG])
        colsb = sbt("colsb", [128, 8]); stmp = sbt("stmp", [128, TG]); sM = sbt("sM", [128, TG]); sEq = sbt("sEq", [128, TG])
        sQ = sbt("sQ", [128, TG]); kd = sbt("kd", [128, 4]); skhs = sbt("skhs", [128, 4])
        skm = sbt("skm", [128, 4, 128], BF16); svb = sbt("svb", [128, 4, 64], BF16); sxd = sbt("sxd", [128, 4, 64])
        one11 = sbt("one11", [1, 1]); ones1 = sbt("ones1", [1, 128])
        em.op("pool", lambda e: e.memset(one11[:], 1.0), writes=[one11])
        em.op("pool", lambda e: e.memset(ones1[:], 1.0), writes=[ones1])
        hq = sbt("hq", [64, TG]); hf = sbt("hf", [64, TG]); hi = sbt("hi", [128, 4, 64]); hvb = sbt("hvb", [128, 4, 64], BF16)
        hsig = sbt("hsig", [64, TG]); hlf = sbt("hlf", [64, TG]); hkk = sbt("hkk", [64, TG]); hb = sbt("hb", [64, TG])
        hE = sbt("hE", [64, TG]); hEn = sbt("hEn", [64, TG]); hA = sbt("hA", [64, TG]); hBt = sbt("hBt", [64, TG])
        hdf = sbt("hdf", [64, TG]); hkh = sbt("hkh", [64, TG], BF16); hkm = sbt("hkm", [128, 32, 64], BF16)

        for g in range(NG):
            seg = (g * TG) // SEG
            off = g * TG - seg * SEG
            otile = otiles[g % 2]

            def XT(r0, n, seg=seg, off=off):
                return xt[seg, r0:r0 + n, off:off + TG]

            def XM(c0, seg=seg, off=off):
                return xm[seg, off:off + TG, c0:c0 + 64].rearrange("(t p) c -> p t c", p=128)

            em.dma("sp", fq[:], XT(R_FQ, 64), writes=[fq])
            em.dma("sp", fk[:], XT(R_FK, 64), writes=[fk])
            em.dma("sp", lfin[64:65, :], XT(R_FF, 1), writes=[lfin])
            em.dma("sp", fv[:], XM(0), writes=[fv])
            em.op("act", lambda e: e.activation(out=spv[64:65, :], in_=lfin[64:65, :], func=AF.Exp, scale=-1.0, bias=nfb[64:65, 0:1]),
                  reads=[lfin, nfb], writes=[spv])
            em.op("act", lambda e: e.activation(out=spv[64:65, :], in_=spv[64:65, :], func=AF.Ln, bias=1.0), reads=[spv], writes=[spv])
            em.op("dve", lambda e: e.tensor_tensor_scan(out=ncr[64:65, :], data0=ones65[64:65, :], data1=spv[64:65, :],
                                                        initial=prevc[64:65, 0:1], op0=ALU.mult, op1=ALU.add),
                  reads=[ones65, spv, prevc], writes=[ncr])
            em.op("act", lambda e: e.copy(out=prevc[64:65, :], in_=ncr[64:65, TG - 1:TG]), reads=[ncr], writes=[prevc])
            em.op("dve", lambda e: e.tensor_scalar(out=QaT[64:65, :], in0=ncr[64:65, :], scalar1=ncr[64:65, TG - 1:TG], scalar2=-1.0,
                                                   op0=ALU.subtract, op1=ALU.mult), reads=[ncr], writes=[QaT])
            em.op("act", lambda e: e.mul(out=QaT[0:64, :], in_=fq[:], mul=0.125), reads=[fq], writes=[QaT])
            em.op("pool", lambda e, g=g: e.tensor_copy(out=KaT[0:64, g * TG:(g + 1) * TG], in_=fk[:]), reads=[fk], writes=[KaT])
            em.op("pool", lambda e, g=g: e.tensor_copy(out=Vaug[:, 4 * g:4 * g + 4, 0:64], in_=fv[:]), reads=[fv], writes=[Vaug])
            for j in range(4):
                em.op("pe", lambda e, j=j: e.matmul(psX[:, 300 + j:301 + j], lhsT=ncr[64:65, j * 128:(j + 1) * 128], rhs=ones65[64:65, 0:1],
                                                    start=True, stop=True), reads=[ncr, ones65], writes=[psX])
            em.op("pe", lambda e: e.matmul(psX[:, 304:305], lhsT=ones65[64:65, 0:128], rhs=ncr[64:65, TG - 1:TG], start=True, stop=True),
                  reads=[ncr, ones65], writes=[psX])
            em.op("dve", lambda e, g=g: e.tensor_copy(out=negc[:, 4 * g:4 * g + 4], in_=psX[:, 300:304]), reads=[psX], writes=[negc])
            em.op("dve", lambda e: e.tensor_copy(out=cbg[:], in_=psX[:, 304:305]), reads=[psX], writes=[cbg])
            nkb = 4 * g + 4
            em.op("dve", lambda e, nkb=nkb: e.tensor_scalar(out=biasg[:, 0:nkb], in0=negc[:, 0:nkb], scalar1=cbg[:, 0:1], scalar2=None,
                                                            op0=ALU.subtract), reads=[negc, cbg], writes=[biasg])
            for kb in range(nkb):
                ps = psS[kb % 2]
                pt = PTf[kb % 2]
                em.op("pe", lambda e, kb=kb, ps=ps: e.matmul(ps[:], lhsT=KaT[0:65, kb * 128:(kb + 1) * 128], rhs=QaT[0:65, :],
                                                            start=True, stop=True), reads=[KaT, QaT], writes=[ps])
                if kb >= 4 * g:
                    j = kb - 4 * g
                    em.op("dve", lambda e, j=j, ps=ps: e.tensor_tensor(out=ftmp[:], in0=ps[:], in1=fmk[:, j, :], op=ALU.add),
                          reads=[ps, fmk], writes=[ftmp])
                    em.op("act", lambda e, kb=kb, pt=pt: e.activation(out=pt[:], in_=ftmp[:], func=AF.Exp, bias=biasg[:, kb:kb + 1]),
                          reads=[ftmp, biasg], writes=[pt])
                else:
                    em.op("act", lambda e, kb=kb, pt=pt, ps=ps: e.activation(out=pt[:], in_=ps[:], func=AF.Exp, bias=biasg[:, kb:kb + 1]),
                          reads=[ps, biasg], writes=[pt])
                em.op("pe", lambda e, kb=kb, pt=pt, nkb=nkb: e.matmul(psO[0:65, :], lhsT=Vaug[:, kb, 0:65], rhs=pt[:],
                                                                     start=(kb == 0), stop=(kb == nkb - 1)),
                      reads=[Vaug, pt], writes=[psO])
            em.op("act", lambda e: e.copy(out=OT[:], in_=psO[0:65, :]), reads=[psO], writes=[OT])
            for t in range(4):
                em.op("pe", lambda e, t=t: e.transpose(psX[:, t * 66:t * 66 + 65], OT[0:65, t * 128:(t + 1) * 128], idf[0:65, 0:65]),
                      reads=[OT, idf], writes=[psX])
            psF = psX[:, 0:264].rearrange("p (t c) -> p t c", c=66)
            em.op("dve", lambda e: e.reciprocal(out=rec[:].rearrange("p (t o) -> p t o", o=1), in_=psF[:, :, 64:65]), reads=[psX], writes=[rec])
            for t in range(4):
                em.op("dve", lambda e, t=t: e.tensor_scalar(out=otile[:, t, 0:64], in0=psF[:, t, 0:64], scalar1=rec[:, t:t + 1], scalar2=None,
                                                            op0=ALU.mult), reads=[psX, rec], writes=[otile])

            em.dma("sp", rq[:], XT(R_RQ, 64), writes=[rq]); em.dma("sp", rqs[:], XT(R_RQS, 64), writes=[rqs])
            em.dma("sp", rk[:], XT(R_RK, 64), writes=[rk]); em.dma("sp", rks[:], XT(R_RKS, 64), writes=[rks])
            em.dma("sp", cst[:], cosT[:, g * TG:(g + 1) * TG], writes=[cst]); em.dma("sp", snt[:], sinsT[:, g * TG:(g + 1) * TG], writes=[snt])
            em.dma("sp", rv[:], XM(64), writes=[rv])
            em.op("pool", lambda e: e.tensor_tensor(out=rt1[:], in0=rq[:], in1=cst[:], op=ALU.mult), reads=[rq, cst], writes=[rt1])
            em.op("pool", lambda e: e.tensor_tensor(out=rt2[:], in0=rqs[:], in1=snt[:], op=ALU.mult), reads=[rqs, snt], writes=[rt2])
            em.op("pool", lambda e: e.tensor_tensor(out=rqf[:], in0=rt1[:], in1=rt2[:], op=ALU.add), reads=[rt1, rt2], writes=[rqf])
            em.op("act", lambda e: e.copy(out=rA[:], in_=rqf[:]), reads=[rqf], writes=[rA])
            em.op("dve", lambda e: e.tensor_tensor(out=rQ[:], in0=rqf[:], in1=retEq[:], op=ALU.mult), reads=[rqf, retEq], writes=[rQ])
            em.op("pool", lambda e: e.tensor_tensor(out=rt1[:], in0=rk[:], in1=cst[:], op=ALU.mult), reads=[rk, cst], writes=[rt1])
            em.op("pool", lambda e: e.tensor_tensor(out=rt2[:], in0=rks[:], in1=snt[:], op=ALU.mult), reads=[rks, snt], writes=[rt2])
            em.op("pool", lambda e: e.tensor_tensor(out=rB[:], in0=rt1[:], in1=rt2[:], op=ALU.add), reads=[rt1, rt2], writes=[rB])
            em.op("pool", lambda e: e.tensor_copy(out=rvb[:], in_=rv[:]), reads=[rv], writes=[rvb])
            for t in range(4):
                em.op("pe", lambda e, t=t: e.transpose(prep[:, t * 32:(t + 1) * 32].bitcast(BF16), rB[:, t * 128:(t + 1) * 128], idb[0:64, 0:64]),
                      reads=[rB, idb], writes=[prep])
            em.op("dve", lambda e: e.tensor_scalar(out=rkm[:], in0=prep[:, 0:128].bitcast(BF16).rearrange("p (t c) -> p t c", c=64),
                                                   scalar1=retkhs[:, 0:1], scalar2=None, op0=ALU.mult), reads=[prep, retkhs], writes=[rkm])
            P["Mkey"] = retM; P["Dkey"] = retDl
            for t in range(4):
                rec_tile(cx, P, rA, rB, rQ, retM[:], rvb, rkm, lambda t, c: retDl[:, 0:1], St_r, 64, 1, t, otile, 64)

            for (tin, r0, n) in ((sxin, R_SX, 64), (sbin, R_SB, 128), (scin, R_SC, 128)):
                em.dma("sp", tin[:, 3:TG + 3], XT(r0, n), writes=[tin])
                if g == 0:
                    em.op("pool", lambda e, tin=tin: e.memset(tin[:, 0:3], 0.0), writes=[tin])
                elif off == 0:
                    em.dma("sp", tin[:, 0:3], xt[seg - 1, r0:r0 + n, SEG - 3:SEG], writes=[tin])
                else:
                    em.dma("sp", tin[:, 0:3], xt[seg, r0:r0 + n, off - 3:off], writes=[tin])
            em.dma("sp", sdt[:], XT(R_SDT, 1), writes=[sdt])
            for (tin, acc, cw, cb_, n, outs) in ((sxin, sxa, cwx, cbx, 64, [xsT]), (sbin, sba, cwb, cbb, 128, [sB]), (scin, sca, cwc, cbc, 128, [sA, sCf])):
                em.op("dve", lambda e, tin=tin, acc=acc, cw=cw: e.tensor_scalar(out=acc[:], in0=tin[:, 0:TG], scalar1=cw[:, 0:1], scalar2=None, op0=ALU.mult),
                      reads=[tin, cw], writes=[acc])
                for k in range(1, 4):
                    em.op("dve", lambda e, k=k, tin=tin, acc=acc, cw=cw: e.scalar_tensor_tensor(
                        out=acc[:], in0=tin[:, k:TG + k], scalar=cw[:, k:k + 1], in1=acc[:], op0=ALU.mult, op1=ALU.add),
                        reads=[tin, cw, acc], writes=[acc])
                for o_ in outs:
                    em.op("act", lambda e, acc=acc, cb_=cb_, o_=o_: e.activation(out=o_[:], in_=acc[:], func=AF.Silu, bias=cb_[:, 0:1]),
                          reads=[acc, cb_], writes=[o_])
            em.op("act", lambda e: e.activation(out=dtr[:], in_=sdt[:], func=AF.Exp, bias=dtb[:, 0:1]), reads=[sdt, dtb], writes=[dtr])
            em.op("act", lambda e: e.activation(out=dtr[:], in_=dtr[:], func=AF.Ln, bias=1.0), reads=[dtr], writes=[dtr])
            em.op("dve", lambda e: e.tensor_scalar(out=gr[:], in0=dtr[:], scalar1=a_neg[:, 0:1], scalar2=None, op0=ALU.mult), reads=[dtr, a_neg], writes=[gr])
            em.op("dve", lambda e: e.tensor_tensor_scan(out=br[:], data0=sm128[:], data1=gr[:], initial=0.0, op0=ALU.mult, op1=ALU.add),
                  reads=[sm128, gr], writes=[br])
            em.op("pe", lambda e: e.matmul(bcb[:], lhsT=ones1[0:1, :], rhs=br[0:1, :], start=True, stop=True), reads=[ones1, br], writes=[bcb])
            for j in range(4):
                em.op("pe", lambda e, j=j: e.matmul(prep[:, 200 + j:201 + j], lhsT=br[0:1, j * 128:(j + 1) * 128], rhs=one11[0:1, 0:1], start=True, stop=True),
                      reads=[br, one11], writes=[prep])
                em.op("pe", lambda e, j=j: e.matmul(prep[:, 204 + j:205 + j], lhsT=dtr[0:1, j * 128:(j + 1) * 128], rhs=one11[0:1, 0:1], start=True, stop=True),
                      reads=[dtr, one11], writes=[prep])
            em.op("dve", lambda e: e.tensor_copy(out=colsb[:], in_=prep[:, 200:208]), reads=[prep], writes=[colsb])
            for j in range(4):
                em.op("dve", lambda e, j=j: e.scalar_tensor_tensor(out=stmp[:, j * 128:(j + 1) * 128], in0=bcb[:, j * 128:(j + 1) * 128],
                                                                   scalar=colsb[:, j:j + 1], in1=mneg[:], op0=ALU.subtract, op1=ALU.add),
                      reads=[bcb, colsb, mneg], writes=[stmp])
            em.op("act", lambda e: e.activation(out=sM[:], in_=stmp[:], func=AF.Exp), reads=[stmp], writes=[sM])
            em.op("act", lambda e: e.activation(out=sEq[:], in_=bcb[:], func=AF.Exp), reads=[bcb], writes=[sEq])
            em.op("dve", lambda e: e.tensor_tensor(out=sQ[:], in0=sCf[:], in1=sEq[:], op=ALU.mult), reads=[sCf, sEq], writes=[sQ])
            em.op("dve", lambda e: e.tensor_tensor(out=kd[:], in0=bcb[:].rearrange("p (j c) -> p j c", c=128)[:, :, 127:128].rearrange("p j o -> p (j o)"),
                                                   in1=colsb[:, 0:4], op=ALU.subtract), reads=[bcb, colsb], writes=[kd])
            em.op("act", lambda e: e.activation(out=skhs[:], in_=kd[:], func=AF.Exp), reads=[kd], writes=[skhs])
            for t in range(4):
                em.op("pe", lambda e, t=t: e.transpose(prep[:, 0:64].bitcast(BF16), sB[:, t * 128:(t + 1) * 128], idb[:]), reads=[sB, idb], writes=[prep])
                em.op("dve", lambda e, t=t: e.tensor_scalar(out=skm[:, t, :], in0=prep[:, 0:64].bitcast(BF16), scalar1=skhs[:, t:t + 1], scalar2=None, op0=ALU.mult),
                      reads=[prep, skhs], writes=[skm])
                em.op("pe", lambda e, t=t: e.transpose(prep[:, 64:128], xsT[:, t * 128:(t + 1) * 128], idf[0:64, 0:64]), reads=[xsT, idf], writes=[prep])
                em.op("dve", lambda e, t=t: e.tensor_scalar(out=svb[:, t, :], in0=prep[:, 64:128], scalar1=colsb[:, 4 + t:5 + t], scalar2=None, op0=ALU.mult),
                      reads=[prep, colsb], writes=[svb])
                em.op("dve", lambda e, t=t: e.tensor_scalar(out=sxd[:, t, :], in0=prep[:, 64:128], scalar1=ssdd[:, 0:1], scalar2=None, op0=ALU.mult),
                      reads=[prep, ssdd], writes=[sxd])
            P["Mkey"] = sM; P["Dkey"] = sEq
            for t in range(4):
                rec_tile(cx, P, sA, sB, sQ, sM[:, t * 128:(t + 1) * 128], svb, skm,
                         lambda t, c: sEq[:, t * 128 + 127:t * 128 + 128], St_s, 128, 1, t, otile, 128, extra=sxd)

            em.dma("sp", hq[:], XT(R_HQ, 64), writes=[hq]); em.dma("sp", hf[:], XT(R_HF, 64), writes=[hf])
            em.dma("sp", hi[:], XM(128), writes=[hi])
            em.op("act", lambda e: e.activation(out=hsig[:], in_=hf[:], func=AF.Sigmoid), reads=[hf], writes=[hsig])
            em.op("dve", lambda e: e.tensor_scalar(out=hsig[:], in0=hsig[:], scalar1=oml[:, 0:1], scalar2=lb[:, 0:1], op0=ALU.mult, op1=ALU.add),
                  reads=[hsig, oml, lb], writes=[hsig])
            em.op("act", lambda e: e.activation(out=hlf[:], in_=hsig[:], func=AF.Ln), reads=[hsig], writes=[hlf])
            em.op("pool", lambda e: e.tensor_scalar(out=hkk[:], in0=hsig[:], scalar1=-1.0, scalar2=1.0, op0=ALU.mult, op1=ALU.add), reads=[hsig], writes=[hkk])
            em.op("dve", lambda e: e.tensor_tensor_scan(out=hb[:], data0=sm16[:], data1=hlf[:], initial=0.0, op0=ALU.mult, op1=ALU.add),
                  reads=[sm16, hlf], writes=[hb])
            em.op("act", lambda e: e.activation(out=hE[:], in_=hb[:], func=AF.Exp), reads=[hb], writes=[hE])
            em.op("act", lambda e: e.activation(out=hEn[:], in_=hb[:], func=AF.Exp, scale=-1.0), reads=[hb], writes=[hEn])
            em.op("pool", lambda e: e.tensor_tensor(out=hA[:], in0=hq[:], in1=hE[:], op=ALU.mult), reads=[hq, hE], writes=[hA])
            em.op("pool", lambda e: e.tensor_tensor(out=hBt[:], in0=hkk[:], in1=hEn[:], op=ALU.mult), reads=[hkk, hEn], writes=[hBt])
            hb3 = hb[:].rearrange("p (c j) -> p c j", j=16)
            em.op("dve", lambda e: e.tensor_tensor(out=hdf[:].rearrange("p (c j) -> p c j", j=16), in0=hb3[:, :, 15:16].to_broadcast([64, TG // 16, 16]),
                                                   in1=hb3, op=ALU.subtract), reads=[hb], writes=[hdf])
            em.op("act", lambda e: e.activation(out=hdf[:], in_=hdf[:], func=AF.Exp), reads=[hdf], writes=[hdf])
            em.op("pool", lambda e: e.tensor_tensor(out=hkh[:], in0=hkk[:], in1=hdf[:], op=ALU.mult), reads=[hkk, hdf], writes=[hkh])
            em.op("pool", lambda e: e.tensor_copy(out=hvb[:], in_=hi[:]), reads=[hi], writes=[hvb])
            for t in range(4):
                em.op("pe", lambda e, t=t: e.transpose(prep[:, 128:160].bitcast(BF16), hkh[:, t * 128:(t + 1) * 128], idb[0:64, 0:64]), reads=[hkh, idb], writes=[prep])
                for c in range(8):
                    em.op("dve", lambda e, t=t, c=c: e.tensor_scalar(out=hkm[:, t * 8 + c, :], in0=prep[:, 128:160].bitcast(BF16), scalar1=sel[:, c:c + 1],
                                                                     scalar2=None, op0=ALU.mult), reads=[prep, sel], writes=[hkm])
            P["Mkey"] = hgM; P["Dkey"] = hE
            for t in range(4):
                rec_tile(cx, P, hA, hBt, hA, hgM[:], hvb, hkm,
                         lambda t, c: hE[:, t * 128 + c * 16 + 15:t * 128 + c * 16 + 16], St_h, 64, 8, t, otile, 192)

            em.dma("sp", o_d[g * TG:(g + 1) * TG, :].rearrange("(t p) c -> p t c", p=128), otile[:], reads=[otile])
        em.finish("sp")
        print("B: nins", em.nins, "nwaits", em.nwaits)
    return nc


def mixer_consts(S_, head):
    f = np.float32
    c = {}
    c["ident"] = np.eye(128, dtype=f)
    half = 32
    inv_freq = (np.float32(10000.0) ** (-np.arange(half, dtype=f) / np.float32(half))).astype(f)
    ang = (np.arange(S_, dtype=f)[:, None] * inv_freq[None, :]).astype(f)
    cos, sin = np.cos(ang).astype(f), np.sin(ang).astype(f)
    c["cosT"] = np.ascontiguousarray(np.concatenate([cos, cos], axis=1).T)
    c["sinsT"] = np.ascontiguousarray(np.concatenate([-sin, sin], axis=1).T)
    s = np.arange(128)[:, None]
    t = np.arange(TG)[None, :]
    c["fmask"] = np.stack([np.where(128 * j + s <= t, 0.0, NEG) for j in range(4)]).astype(f)
    lg = np.log1p(-np.exp2(-5.0 - head))
    t1 = np.arange(128)[None, :]
    c["retM"] = np.where(s <= t1, np.exp(lg * (t1 - s)) / 8.0, 0.0).astype(f)
    c["retEq"] = np.broadcast_to(np.exp(lg * ((np.arange(TG) % 128) + 1.0))[None, :], (64, TG)).astype(f).copy()
    c["retkhs"] = (np.exp(lg * (127.0 - np.arange(128))) / 8.0).astype(f)[:, None].copy()
    c["retDl"] = np.full((64, 1), np.exp(lg * 128.0), dtype=f)
    c["mneg"] = np.where(s <= t1, 0.0, NEG).astype(f)
    c["sm128"] = (np.arange(TG) % 128 != 0).astype(f)[None, :].copy()
    c["hgM"] = np.where((s <= t1) & (s // 16 == t1 // 16), 1.0, 0.0).astype(f)
    c["sm16"] = np.broadcast_to((np.arange(TG) % 16 != 0).astype(f)[None, :], (64, TG)).copy()
    c["sel"] = (np.arange(128)[:, None] // 16 == np.arange(8)[None, :]).astype(f)
    return c


def build_C(tpc):
    nc = bass.Bass("TRN2", target_bir_lowering=False)
    ng = tpc // TG
    with ExitStack() as es:
        cx = Cx(nc, es)
        em = cx.em
        h1 = cx.din("h1", [tpc, D])
        y_d = cx.din("y", [4, tpc, 256])
        gg_d = cx.din("gg", [tpc, 768])
        p_d = cx.din("p", [tpc, 256])
        retn = cx.din("ret_norm", [256]); ssdn = cx.din("ssd_norm", [256]); hgn = cx.din("hgrn_norm", [256])
        wout = cx.din("w_out", [D, D])
        n2 = cx.din("ffn2_norm", [D]); wup = cx.din("ffn2_w_up", [D, 2 * DFF]); wdn = cx.din("ffn2_w_down", [DFF, D])
        plen = cx.din("ple_norm", [D]); wgate = cx.din("ple_w_gate", [D, D]); wproj = cx.din("ple_w_proj", [256, D])
        finn = cx.din("final_norm", [D])
        ident = cx.din("ident", [128, 128])
        h_out = cx.dout("h_out", [tpc, D])
        out_n = cx.dout("out_n", [tpc, D])

        sh = ffn_shared(cx)
        idf, idb = load_ident(cx, ident)
        sh["identb"] = idb
        ffn = FFN(cx, "f2", n2, wup, wdn, sh)
        retg = load_bcast(cx, "retg", retn, 256); ssdg = load_bcast(cx, "ssdg", ssdn, 256); hgg = load_bcast(cx, "hgg", hgn, 256)
        pleg = load_bcast(cx, "pleg", plen, D); fing = load_bcast(cx, "fing", finn, D)
        wpr = cx.sb("wpr", [128, 2, D], BF16)
        em.dma("pool", wpr[:], wproj.rearrange("(k p) c -> p k c", p=128), writes=[wpr])
        hg = cx.sb("hg", [128, 4, D], F32)
        yin = cx.sb("yin", [128, 4, 256], F32)
        ggt = cx.sb("ggt", [128, 768], F32)
        sgt = cx.sb("sgt", [128, 768], F32)
        pt = cx.sb("pt", [128, 256], F32)
        ptb = cx.sb("ptb", [128, 256], BF16)
        pT = cx.sb("pT", [128, 2, TG], BF16)
        ybf = cx.sb("ybf", [128, D], BF16)
        tmp = cx.sb("ytmp", [128, 256], F32)
        zt = cx.sb("zt", [128, 256], F32)
        ssq4 = cx.sb("ssq4", [128, 4], F32)
        fo = cx.sb("fo", [128, D], F32)
        uT = sh["uT"]
        per_g = [wout[:, 0:512], wout[:, 512:1024]] + ffn.blocks() + [wgate[:, 0:512], wgate[:, 512:1024]]
        ws = WStream(cx, per_g * ng)
        scr = sh["scr"]

        def headnorm(m, gain, gcol, t):
            yv = yin[:, :, m * 64:(m + 1) * 64]
            t3 = tmp[:].rearrange("p (j e) -> p j e", e=64)
            em.op("pool", lambda e: e.tensor_tensor(out=t3, in0=yv, in1=yv, op=ALU.mult), reads=[yin], writes=[tmp])
            em.op("dve", lambda e: e.tensor_reduce(out=ssq4[:], in_=t3, axis=AX.X, op=ALU.add), reads=[tmp], writes=[ssq4])
            em.op("dve", lambda e: e.tensor_scalar(out=ssq4[:], in0=ssq4[:], scalar1=1.0 / 64, scalar2=EPS, op0=ALU.mult, op1=ALU.add),
                  reads=[ssq4], writes=[ssq4])
            em.op("act", lambda e: e.activation(out=ssq4[:], in_=ssq4[:], func=AF.Sqrt), reads=[ssq4], writes=[ssq4])
            em.op("dve", lambda e: e.reciprocal(out=ssq4[:], in_=ssq4[:]), reads=[ssq4], writes=[ssq4])
            em.op("dve", lambda e: e.tensor_tensor(out=t3, in0=yv, in1=ssq4[:].rearrange("p (j o) -> p j o", o=1).to_broadcast([128, 4, 64]),
                                                   op=ALU.mult), reads=[yin, ssq4], writes=[tmp])
            em.op("pool", lambda e: e.tensor_tensor(out=tmp[:], in0=tmp[:], in1=gain[:], op=ALU.mult), reads=[tmp, gain], writes=[tmp])
            em.op("dve", lambda e: e.tensor_tensor(out=ybf[:, m * 256:(m + 1) * 256], in0=tmp[:], in1=sgt[:, gcol:gcol + 256], op=ALU.mult),
                  reads=[tmp, sgt], writes=[ybf])

        for g in range(ng):
            wbase = g * len(per_g)
            em.dma("sp", hg[:], h1[g * TG:(g + 1) * TG, :].rearrange("(t p) c -> p t c", p=128), writes=[hg])
            for t in range(4):
                r0 = g * TG + t * 128
                em.dma("sp", yin[:], y_d[:, r0:r0 + 128, :].rearrange("j p c -> p j c"), writes=[yin])
                em.dma("sp", ggt[:], gg_d[r0:r0 + 128, :], writes=[ggt])
                em.dma("sp", pt[:], p_d[r0:r0 + 128, :], writes=[pt])
                em.op("act", lambda e: e.activation(out=sgt[:], in_=ggt[:], func=AF.Silu), reads=[ggt], writes=[sgt])
                em.op("act", lambda e: e.copy(out=ybf[:, 0:256].rearrange("p (j e) -> p j e", e=64), in_=yin[:, :, 0:64]), reads=[yin], writes=[ybf])
                headnorm(1, retg, 0, t)
                headnorm(3, hgg, 512, t)
                em.op("pool", lambda e: e.tensor_tensor(out=zt[:].rearrange("p (j e) -> p j e", e=64), in0=yin[:, :, 128:192],
                                                        in1=sgt[:, 256:512].rearrange("p (j e) -> p j e", e=64), op=ALU.mult),
                      reads=[yin, sgt], writes=[zt])
                rms_tm(cx, zt[:], [zt], ssdg[:], [ssdg], ybf[:, 512:768], [ybf], 256, scr)
                transpose_to_T(cx, ybf, uT, t, idb, sh["psT"])
                em.op("pool", lambda e: e.tensor_copy(out=ptb[:], in_=pt[:]), reads=[pt], writes=[ptb])
                transpose_to_T(cx, ptb, pT, t, idb, sh["psT"], nk=2)
            for hf in range(2):
                wb = ws.get(wbase + hf)
                for t in range(4):
                    pd = sh["pd"][(2 * hf + t) % 2]
                    for k in range(8):
                        em.op("pe", lambda e, k=k, t=t, pd=pd, wb=wb: e.matmul(pd[:], lhsT=uT[:, k, t * 128:(t + 1) * 128], rhs=wb[:, k, :],
                                                                               start=(k == 0), stop=(k == 7)), reads=[uT, wb], writes=[pd])
                    em.op("dve", lambda e, t=t, hf=hf, pd=pd: e.tensor_tensor(out=hg[:, t, hf * 512:(hf + 1) * 512], in0=pd[:],
                                                                              in1=hg[:, t, hf * 512:(hf + 1) * 512], op=ALU.add),
                          reads=[pd, hg], writes=[hg])
            ffn.emit(hg, ws, wbase + 2)
            for t in range(4):
                u = sh["u"][t % 2]
                rms_tm(cx, hg[:, t, :], [hg], pleg[:], [pleg], u[:], [u], D, scr)
                transpose_to_T(cx, u, uT, t, idb, sh["psT"])
            for hf in range(2):
                wb = ws.get(wbase + 14 + hf)
                for t in range(4):
                    pg = sh["pg"][t % 2]
                    pu = sh["pu"][t % 2]
                    sg = sh["sg"][t % 2]
                    for k in range(8):
                        em.op("pe", lambda e, k=k, t=t, pg=pg, wb=wb: e.matmul(pg[:], lhsT=uT[:, k, t * 128:(t + 1) * 128], rhs=wb[:, k, :],
                                                                               start=(k == 0), stop=(k == 7)), reads=[uT, wb], writes=[pg])
                    for k in range(2):
                        em.op("pe", lambda e, k=k, t=t, pu=pu, hf=hf: e.matmul(pu[:], lhsT=pT[:, k, t * 128:(t + 1) * 128],
                                                                               rhs=wpr[:, k, hf * 512:(hf + 1) * 512],
                                                                               start=(k == 0), stop=(k == 1)), reads=[pT, wpr], writes=[pu])
                    em.op("act", lambda e, pg=pg, sg=sg: e.activation(out=sg[:], in_=pg[:], func=AF.Sigmoid), reads=[pg], writes=[sg])
                    em.op("dve", lambda e, pu=pu, sg=sg: e.tensor_tensor(out=sg[:], in0=pu[:], in1=sg[:], op=ALU.mult), reads=[pu, sg], writes=[sg])
                    em.op("dve", lambda e, t=t, hf=hf, sg=sg: e.tensor_tensor(out=hg[:, t, hf * 512:(hf + 1) * 512], in0=sg[:],
                                                                              in1=hg[:, t, hf * 512:(hf + 1) * 512], op=ALU.add),
                          reads=[sg, hg], writes=[hg])
            em.dma("sp", h_out[g * TG:(g + 1) * TG, :].rearrange("(t p) c -> p t c", p=128), hg[:], reads=[hg])
            for t in range(4):
                rms_tm(cx, hg[:, t, :], [hg], fing[:], [fing], fo[:], [fo], D, scr)
                r0 = g * TG + t * 128
                em.dma("sp", out_n[r0:r0 + 128, :], fo[:], reads=[fo])
        em.finish("sp")
        print("C: nins", em.nins, "nwaits", em.nwaits)
    return nc


def _win_layout():
    sizes = (256,) * 3 + (4,) + (256,) * 4 + (256, 512, 4) + (256,) * 4
    off = np.concatenate([[0], np.cumsum(sizes)])
    names = "fq fk fv ff rq rk rv rg sz sxbc sdt hq hf hi hg".split()
    col = {n: int(off[i]) for i, n in enumerate(names)}

    def hc(n, j, sw=False):
        idx = np.arange(col[n] + 64 * j, col[n] + 64 * j + 64)
        return np.concatenate([idx[32:], idx[:32]]) if sw else idx
    fm, tm = [], []
    sx0 = col["sxbc"]
    for j in range(4):
        fm.append(np.concatenate([hc("fq", j), hc("fk", j), hc("rq", j), hc("rq", j, True), hc("rk", j), hc("rk", j, True),
                                  np.arange(sx0 + 64 * j, sx0 + 64 * j + 64), np.arange(sx0 + 256, sx0 + 512),
                                  hc("hq", j), hc("hf", j), [col["ff"] + j], [col["sdt"] + j]]))
        tm.append(np.concatenate([hc("fv", j), hc("rv", j), hc("hi", j)]))
    tm.append(np.arange(col["rg"], col["rg"] + 256))
    tm.append(np.arange(col["sz"], col["sz"] + 256))
    tm.append(np.arange(col["hg"], col["hg"] + 256))
    return np.concatenate(fm), np.concatenate(tm)


_PROGS = {}


def _prog(name, fn, arg):
    k = (name, arg)
    if k not in _PROGS:
        _PROGS[k] = fn(arg)
    return _PROGS[k]


def kernel(x, p, ffn1_norm, ffn1_w_up, ffn1_w_down, mix_norm, w_in, fox_f_bias, ret_norm, conv_w, conv_b, dt_bias,
           a_log, ssd_d, ssd_norm, hgrn_lower_bounds, hgrn_norm, w_out, ffn2_norm, ffn2_w_up, ffn2_w_down,
           ple_norm, ple_w_gate, ple_w_proj, final_norm):
    f = np.float32
    A = lambda a: np.ascontiguousarray(np.asarray(a, dtype=f))
    x, p = np.asarray(x, dtype=f), np.asarray(p, dtype=f)
    Bn, S_, _ = x.shape
    depth = p.shape[0]
    tpc = S_ // 4
    cores = list(range(8))
    fm_idx, tm_idx = _win_layout()
    ident = np.eye(128, dtype=f)
    ncA = _prog("A", build_A, tpc)
    ncB = _prog("B", build_B, S_)
    ncC = _prog("C", build_C, tpc)
    consts = [mixer_consts(S_, j) for j in range(4)]
    h = [A(x[c // 4, (c % 4) * tpc:(c % 4 + 1) * tpc]) for c in cores]
    out = None
    for l in range(depth):
        wl = np.asarray(w_in[l], dtype=f)
        w_fm, w_tm = A(wl[:, fm_idx]), A(wl[:, tm_idx])
        wa = dict(ffn1_norm=A(ffn1_norm[l]), ffn1_w_up=A(ffn1_w_up[l]), ffn1_w_down=A(ffn1_w_down[l]), mix_norm=A(mix_norm[l]),
                  w_fm=w_fm, w_tm=w_tm, ident=ident)
        resA = run_bass_kernel_spmd(ncA, [dict(wa, h_in=h[c]) for c in cores], core_ids=cores).results
        cw = np.asarray(conv_w[l], dtype=f)
        cb = np.asarray(conv_b[l], dtype=f)
        hl = np.asarray(hgrn_lower_bounds, dtype=f)
        lbsel = np.zeros((64, hl.shape[0]), dtype=f)
        lbsel[:, 1:l + 1] = 1.0
        mapsB = []
        for c in cores:
            b, j = c // 4, c % 4
            m = dict(consts[j])
            m["xt"] = A(np.stack([resA[4 * b + s]["xt"][j * NFM:(j + 1) * NFM] for s in range(4)]))
            m["xm"] = A(np.stack([resA[4 * b + s]["xm"][:, j * NTM:(j + 1) * NTM] for s in range(4)]))
            m["fox_fb"] = A(fox_f_bias[l][j:j + 1]).reshape(1, 1)
            m["cw_x"] = A(cw[:, 64 * j:64 * j + 64].T); m["cw_b"] = A(cw[:, 256:384].T); m["cw_c"] = A(cw[:, 384:512].T)
            m["cb_x"] = A(cb[64 * j:64 * j + 64]).reshape(64, 1); m["cb_b"] = A(cb[256:384]).reshape(128, 1); m["cb_c"] = A(cb[384:512]).reshape(128, 1)
            m["dt_bias"] = A(dt_bias[l][j:j + 1]).reshape(1, 1); m["a_log"] = A(a_log[l][j:j + 1]).reshape(1, 1)
            m["ssd_d"] = A(ssd_d[l][j:j + 1])
            m["hlb"] = A(hl[:, 64 * j:64 * j + 64].T)
            m["lbsel"] = lbsel
            mapsB.append(m)
        resB = run_bass_kernel_spmd(ncB, mapsB, core_ids=cores).results
        wc = dict(ret_norm=A(ret_norm[l]), ssd_norm=A(ssd_norm[l]), hgrn_norm=A(hgrn_norm[l]), w_out=A(w_out[l]),
                  ffn2_norm=A(ffn2_norm[l]), ffn2_w_up=A(ffn2_w_up[l]), ffn2_w_down=A(ffn2_w_down[l]), ple_norm=A(ple_norm[l]),
                  ple_w_gate=A(ple_w_gate[l]), ple_w_proj=A(ple_w_proj[l]), final_norm=A(final_norm), ident=ident)
        mapsC = []
        for c in cores:
            b, s = c // 4, c % 4
            m = dict(wc)
            m["h1"] = resA[c]["h1"]
            m["y"] = A(np.stack([resB[4 * b + j]["o"][s * tpc:(s + 1) * tpc] for j in range(4)]))
            m["gg"] = resA[c]["gg"]
            m["p"] = A(p[l, b, s * tpc:(s + 1) * tpc])
            mapsC.append(m)
        resC = run_bass_kernel_spmd(ncC, mapsC, core_ids=cores).results
        h = [resC[c]["h_out"] for c in cores]
        out = [resC[c]["out_n"] for c in cores]
    res = np.stack([np.concatenate([out[4 * b + s] for s in range(4)], axis=0) for b in range(Bn)])
    return res.astype(f)
```
